# Optimizing a Trainium2 kernel written in Bass

```python
import jax, jax.numpy as jnp
from jax import lax
import numpy as np

D_MODEL = 1024
BATCH = 8
SEQ = 2048
DEPTH = 1

MIX_WIDTH = D_MODEL
HEAD_DIM = 64
ATTN_WIDTH = MIX_WIDTH // 2
N_ATTN_HEADS = ATTN_WIDTH // HEAD_DIM
DILATED_BRANCHES = ((128, 1), (512, 4), (2048, 16))
Q_BLOCK = 128
ROPE_THETA = 10000.0
SSD_WIDTH = MIX_WIDTH - ATTN_WIDTH
SSD_HEAD_DIM = 64
N_SSD_HEADS = SSD_WIDTH // SSD_HEAD_DIM
N_SSD_GROUPS = 2
SSD_STATE = 128
CONV_WIDTH = 4
CHUNK = 128
D_FF = 4 * D_MODEL
EPS = 1e-6
CONV_CHANNELS = SSD_WIDTH + 2 * N_SSD_GROUPS * SSD_STATE
IN_PROJ_WIDTH = 3 * ATTN_WIDTH + SSD_WIDTH + CONV_CHANNELS + N_SSD_HEADS

kernel_name = "hybrid_dilated_attn_ssd_block"


def rms_norm(x, w):
    xf = x.astype(jnp.float32)
    y = xf * lax.rsqrt(jnp.mean(xf * xf, axis=-1, keepdims=True) + EPS)
    return (y * w.astype(jnp.float32)).astype(x.dtype)


def rope(x):
    s, d = x.shape[1], x.shape[-1]
    half = d // 2
    inv_freq = ROPE_THETA ** (-jnp.arange(half, dtype=jnp.float32) / half)
    ang = jnp.arange(s, dtype=jnp.float32)[:, None] * inv_freq[None, :]
    cos = jnp.cos(ang)[None, :, None, :]
    sin = jnp.sin(ang)[None, :, None, :]
    xf = x.astype(jnp.float32)
    x1, x2 = xf[..., :half], xf[..., half:]
    out = jnp.concatenate([x1 * cos - x2 * sin, x2 * cos + x1 * sin], axis=-1)
    return out.astype(x.dtype)


def dilated_attention(q, k, v):
    b, h, s, d = q.shape
    nb = s // Q_BLOCK
    scale = d ** -0.5
    q_blocks = q.reshape(b, h, nb, Q_BLOCK, d).transpose(2, 0, 1, 3, 4)
    t_blocks = jnp.arange(s, dtype=jnp.int32).reshape(nb, Q_BLOCK)

    def block(args):
        qb, tb = args
        outs, lses = [], []
        for window, dil in DILATED_BRANCHES:
            n_keys = window // dil + 1
            idx = tb[:, None] - dil * jnp.arange(n_keys, dtype=jnp.int32)[None, :]
            valid = idx >= 0
            flat = jnp.maximum(idx, 0).reshape(-1)
            kg = jnp.take(k, flat, axis=2).reshape(b, h, Q_BLOCK, n_keys, d)
            vg = jnp.take(v, flat, axis=2).reshape(b, h, Q_BLOCK, n_keys, d)
            sc = jnp.einsum("bhqd,bhqkd->bhqk", qb, kg).astype(jnp.float32) * scale
            sc = jnp.where(valid, sc, -jnp.inf)
            lse = jax.nn.logsumexp(sc, axis=-1)
            p = jnp.exp(sc - lse[..., None])
            o = jnp.einsum("bhqk,bhqkd->bhqd", p.astype(v.dtype), vg)
            outs.append(o.astype(jnp.float32))
            lses.append(lse)
        wts = jax.nn.softmax(jnp.stack(lses, axis=0), axis=0)
        o = jnp.sum(wts[..., None] * jnp.stack(outs, axis=0), axis=0)
        return o.astype(q.dtype)

    o = lax.map(block, (q_blocks, t_blocks))
    return o.transpose(1, 2, 0, 3, 4).reshape(b, h, s, d)


def causal_depthwise_conv(u, w, bias):
    s = u.shape[1]
    up = jnp.pad(u, ((0, 0), (CONV_WIDTH - 1, 0), (0, 0)))
    out = bias
    for tap in range(CONV_WIDTH):
        out = out + up[:, tap:tap + s, :] * w[tap]
    return out


def ssd_scan(x, dt, a, b_mat, c_mat):
    bsz, s, h, p = x.shape
    g, n = b_mat.shape[-2:]
    r = h // g
    nc = s // CHUNK
    xc = (x * dt[..., None]).reshape(bsz, nc, CHUNK, g, r, p)
    a_dt = (dt * a).reshape(bsz, nc, CHUNK, g, r)
    bc = b_mat.reshape(bsz, nc, CHUNK, g, n)
    cc = c_mat.reshape(bsz, nc, CHUNK, g, n)
    a_cs = jnp.cumsum(a_dt, axis=2)
    a_cs_t = jnp.moveaxis(a_cs, 2, -1)
    causal = jnp.tril(jnp.ones((CHUNK, CHUNK), dtype=bool))
    seg = jnp.exp(jnp.where(causal, a_cs_t[..., :, None] - a_cs_t[..., None, :], -jnp.inf))
    cb = jnp.einsum("bclgn,bcsgn->bcgls", cc, bc)
    y_diag = jnp.einsum("bcgls,bcgrls,bcsgrp->bclgrp", cb, seg, xc)
    decay_to_end = jnp.exp(a_cs[:, :, -1:] - a_cs)
    chunk_states = jnp.einsum("bclgn,bclgr,bclgrp->bcgrpn", bc, decay_to_end, xc)
    chunk_decay = jnp.exp(a_cs[:, :, -1])

    def step(state, inp):
        cs, dec = inp
        return dec[..., None, None] * state + cs, state

    init = jnp.zeros((bsz, g, r, p, n), dtype=x.dtype)
    _, states_in = lax.scan(step, init, (jnp.moveaxis(chunk_states, 1, 0),
                                         jnp.moveaxis(chunk_decay, 1, 0)))
    states_in = jnp.moveaxis(states_in, 0, 1)
    y_off = jnp.einsum("bclgn,bcgrpn,bclgr->bclgrp", cc, states_in, jnp.exp(a_cs))
    return (y_diag + y_off).reshape(bsz, s, h, p)


def setup_inputs(seed: int = 0) -> dict:
    key = jax.random.key(seed)
    ks = jax.random.split(key, 16)
    f32 = jnp.float32
    x = jax.random.normal(ks[0], (BATCH, SEQ, D_MODEL), f32)
    attn_norm_w = 1.0 + 0.02 * jax.random.normal(ks[1], (DEPTH, D_MODEL), f32)
    w_in = jax.random.normal(ks[2], (DEPTH, D_MODEL, IN_PROJ_WIDTH), f32) * D_MODEL ** -0.5
    q_norm_w = 1.0 + 0.02 * jax.random.normal(ks[3], (DEPTH, HEAD_DIM), f32)
    k_norm_w = 1.0 + 0.02 * jax.random.normal(ks[4], (DEPTH, HEAD_DIM), f32)
    conv_w = jax.random.normal(ks[5], (DEPTH, CONV_WIDTH, CONV_CHANNELS), f32) * CONV_WIDTH ** -0.5
    conv_b = 0.01 * jax.random.normal(ks[6], (DEPTH, CONV_CHANNELS), f32)
    dt0 = jnp.exp(jax.random.uniform(ks[7], (DEPTH, N_SSD_HEADS), f32,
                                     minval=math_log(0.001), maxval=math_log(0.1)))
    dt_bias = dt0 + jnp.log(-jnp.expm1(-dt0))
    a_log = jnp.log(jax.random.uniform(ks[8], (DEPTH, N_SSD_HEADS), f32, minval=1.0, maxval=16.0))
    d_skip = 1.0 + 0.1 * jax.random.normal(ks[9], (DEPTH, N_SSD_HEADS), f32)
    ssd_norm_w = 1.0 + 0.02 * jax.random.normal(ks[10], (DEPTH, SSD_WIDTH), f32)
    w_out = jax.random.normal(ks[11], (DEPTH, MIX_WIDTH, D_MODEL), f32) * MIX_WIDTH ** -0.5
    mlp_norm_w = 1.0 + 0.02 * jax.random.normal(ks[12], (DEPTH, D_MODEL), f32)
    w_up = jax.random.normal(ks[13], (DEPTH, D_MODEL, D_FF), f32) * D_MODEL ** -0.5
    w_down = jax.random.normal(ks[14], (DEPTH, D_FF, D_MODEL), f32) * D_FF ** -0.5
    return {"x": x, "attn_norm_w": attn_norm_w, "w_in": w_in, "q_norm_w": q_norm_w,
            "k_norm_w": k_norm_w, "conv_w": conv_w, "conv_b": conv_b, "dt_bias": dt_bias,
            "a_log": a_log, "d_skip": d_skip, "ssd_norm_w": ssd_norm_w, "w_out": w_out,
            "mlp_norm_w": mlp_norm_w, "w_up": w_up, "w_down": w_down}


def math_log(v):
    return float(np.log(v))


def reference(x, attn_norm_w, w_in, q_norm_w, k_norm_w, conv_w, conv_b, dt_bias,
              a_log, d_skip, ssd_norm_w, w_out, mlp_norm_w, w_up, w_down):
    b, s, _ = x.shape
    splits = np.cumsum([ATTN_WIDTH, ATTN_WIDTH, ATTN_WIDTH, SSD_WIDTH, CONV_CHANNELS]).tolist()
    for i in range(DEPTH):
        h = rms_norm(x, attn_norm_w[i])
        proj = h @ w_in[i]
        q, k, v, z, xbc, dt_raw = jnp.split(proj, splits, axis=-1)

        q = rope(rms_norm(q.reshape(b, s, N_ATTN_HEADS, HEAD_DIM), q_norm_w[i]))
        k = rope(rms_norm(k.reshape(b, s, N_ATTN_HEADS, HEAD_DIM), k_norm_w[i]))
        v = v.reshape(b, s, N_ATTN_HEADS, HEAD_DIM)
        o_attn = dilated_attention(q.transpose(0, 2, 1, 3), k.transpose(0, 2, 1, 3),
                                   v.transpose(0, 2, 1, 3))
        o_attn = o_attn.transpose(0, 2, 1, 3).reshape(b, s, ATTN_WIDTH)

        xbc = jax.nn.silu(causal_depthwise_conv(xbc, conv_w[i], conv_b[i]))
        xs, bm, cm = jnp.split(xbc, [SSD_WIDTH, SSD_WIDTH + N_SSD_GROUPS * SSD_STATE], axis=-1)
        xs = xs.astype(jnp.float32).reshape(b, s, N_SSD_HEADS, SSD_HEAD_DIM)
        bm = bm.astype(jnp.float32).reshape(b, s, N_SSD_GROUPS, SSD_STATE)
        cm = cm.astype(jnp.float32).reshape(b, s, N_SSD_GROUPS, SSD_STATE)
        dt = jax.nn.softplus(dt_raw.astype(jnp.float32) + dt_bias[i].astype(jnp.float32))
        a = -jnp.exp(a_log[i].astype(jnp.float32))
        y = ssd_scan(xs, dt, a, bm, cm) + d_skip[i].astype(jnp.float32)[:, None] * xs
        y = y.reshape(b, s, SSD_WIDTH) * jax.nn.silu(z.astype(jnp.float32))
        y = rms_norm(y.reshape(b, s, N_SSD_GROUPS, SSD_WIDTH // N_SSD_GROUPS),
                     ssd_norm_w[i].reshape(N_SSD_GROUPS, SSD_WIDTH // N_SSD_GROUPS))
        y = y.reshape(b, s, SSD_WIDTH).astype(x.dtype)

        x = x + jnp.concatenate([o_attn, y], axis=-1) @ w_out[i]

        hm = rms_norm(x, mlp_norm_w[i])
        x = x + jnp.square(jax.nn.relu(hm @ w_up[i])) @ w_down[i]
    return x
```

```python
import numpy as np
import ml_dtypes
import concourse.bass as bass
import concourse.mybir as mybir
from concourse.bass_utils import run_bass_kernel_spmd

F32 = mybir.dt.float32
BF16 = mybir.dt.bfloat16
AF = mybir.ActivationFunctionType
ALU = mybir.AluOpType
AX = mybir.AxisListType

T = 2048
NB = 16
D = 1024
EPS = 1e-6
CELL = 64


def _esz(dt):
    return 2 if dt == BF16 else 4


class Prog:
    def __init__(self, nc):
        self.nc = nc
        self.ops = []
        self.cw = {}
        self.cr = {}
        self.base = {}
        self.last_on = {}
        self.waited = {}
        self.dma_ring = {}
        self.dma_cnt = {}
        self.dma_last = {}

    def rng(self, ap):
        name = ap.tensor.name
        if name not in self.base:
            return None
        space, base = self.base[name]
        esz = _esz(ap.dtype)
        a = ap.ap
        pstride = a[0][0]
        off = ap.offset % pstride if pstride > 0 else ap.offset
        ext = 0
        for st, cnt in a[1:]:
            ext += (cnt - 1) * abs(st)
        lo = base + off * esz
        hi = base + (off + ext + 1) * esz
        if space == 'P':
            return (space, lo // 2048, (hi - 1) // 2048)
        return (space, lo // CELL, (hi - 1) // CELL)

    def add(self, eng, fn, reads, writes, dma=False, nsem=8):
        idx = len(self.ops)
        deps = set()
        rr = [self.rng(a) for a in reads]
        ww = [self.rng(a) for a in writes]
        for r in rr:
            if r is None:
                continue
            sp, lo, hi = r
            for c in range(lo, hi + 1):
                w = self.cw.get((sp, c))
                if w is not None:
                    deps.add(w)
        for r in ww:
            if r is None:
                continue
            sp, lo, hi = r
            for c in range(lo, hi + 1):
                k = (sp, c)
                w = self.cw.get(k)
                if w is not None:
                    deps.add(w)
                rd = self.cr.get(k)
                if rd:
                    deps.update(rd.values())
        rkey = ('d', idx) if dma else eng
        for r in rr:
            if r is None:
                continue
            sp, lo, hi = r
            for c in range(lo, hi + 1):
                self.cr.setdefault((sp, c), {})[rkey] = idx
        for r in ww:
            if r is None:
                continue
            sp, lo, hi = r
            for c in range(lo, hi + 1):
                self.cw[(sp, c)] = idx
                self.cr[(sp, c)] = {}
        waits = []
        best = {}
        for d in deps:
            o = self.ops[d]
            if o['dma']:
                waits.append(d)
                continue
            pe = o['eng']
            if pe == eng and not dma:
                if eng == 'pe':
                    continue
            if d > best.get(pe, -1):
                best[pe] = d
        for pe, d in best.items():
            if self.waited.get((eng, pe), -1) >= d:
                continue
            self.waited[(eng, pe)] = d
            waits.append(d)
        op = dict(eng=eng, fn=fn, waits=waits, dma=dma, sig=False, idx=idx,
                  wcells=ww, rcells=rr)
        if dma:
            ring = self.dma_cnt.setdefault(eng, 0)
            slot = ring % nsem
            self.dma_cnt[eng] = ring + 1
            key = (eng, slot)
            prev = self.dma_last.get(key)
            op['slot'] = key
            op['prev'] = prev
            self.dma_last[key] = idx
            op['sig'] = True
        for d in waits:
            self.ops[d]['sig'] = True
        self.ops.append(op)
        return idx

    def _raw_or_waw(self, prod, rr, ww):
        pw = prod['wcells']
        for r in list(rr) + list(ww):
            if r is None:
                continue
            for q in pw:
                if q is None:
                    continue
                if q[0] == r[0] and not (q[2] < r[1] or r[2] < q[1]):
                    return True
        return False

    def emit(self, final_wait_eng='sp'):
        nc = self.nc
        engs = ['pe', 'act', 'dve', 'pool', 'sp']
        per = {e: [o for o in self.ops if o['eng'] == e] for e in engs}
        semname = {}
        cnt = {}
        for o in self.ops:
            if not o['sig']:
                continue
            if o['dma']:
                k = ('dma',) + o['slot']
                cnt[k] = cnt.get(k, 0) + 16
            else:
                k = ('c', o['eng'])
                cnt[k] = cnt.get(k, 0) + 1
            o['sk'] = k
            o['sv'] = cnt[k]
        keys = sorted(set(o['sk'] for o in self.ops if o['sig']), key=str)
        import contextlib
        with contextlib.ExitStack() as st:
            sems = {k: st.enter_context(nc.semaphore("s_" + "_".join(str(x) for x in k))) for k in keys}
            block = st.enter_context(nc.Block())
            final_dmas = [o for o in self.ops if o['dma'] and o.get('final')]

            def run(e, eng):
                wv = {}
                for o in per[e]:
                    wl = list(o['waits'])
                    if o['dma'] and o['prev'] is not None:
                        wl.append(o['prev'])
                    need = []
                    for d in wl:
                        p = self.ops[d]
                        k, v = p['sk'], p['sv']
                        if wv.get(k, 0) >= v:
                            continue
                        wv[k] = v
                        need.append((k, v))
                    fold = (e in ('act', 'dve', 'pool')) and (not o['dma']) and len(need) > 0
                    for (k, v) in (need[:-1] if fold else need):
                        eng.wait_ge(sems[k], v)
                    ins = o['fn'](eng)
                    if fold:
                        k, v = need[-1]
                        ins._wait_ge(sems[k], v)
                    if o['sig']:
                        ins.then_inc(sems[o['sk']], 16 if o['dma'] else 1)
                if e == final_wait_eng:
                    for o in final_dmas:
                        k, v = o['sk'], o['sv']
                        if wv.get(k, 0) >= v:
                            continue
                        wv[k] = v
                        eng.wait_ge(sems[k], v)

            if per['pe']:
                block.tensor(lambda eng: run('pe', eng))
            if per['act']:
                block.scalar(lambda eng: run('act', eng))
            if per['dve']:
                block.vector(lambda eng: run('dve', eng))
            if per['pool']:
                block.gpsimd(lambda eng: run('pool', eng))
            block.sync(lambda eng: run('sp', eng))

    def mm(self, out, lhsT, rhs, start=True, stop=True, skip=False):
        return self.add('pe', lambda e: e.matmul(out, lhsT, rhs, start=start, stop=stop, skip_group_check=skip),
                        [lhsT, rhs] + ([] if start else [out]), [out])

    def tr(self, out, in_, ident):
        return self.add('pe', lambda e: e.transpose(out, in_, ident), [in_, ident], [out])

    def act(self, out, in_, func, bias=None, scale=None, accum=None):
        kw = {}
        rd = [in_]
        if bias is not None:
            kw['bias'] = bias
            if not isinstance(bias, (int, float)):
                rd.append(bias)
        if scale is not None:
            kw['scale'] = scale
            if not isinstance(scale, (int, float)):
                rd.append(scale)
        wr = [out]
        if accum is not None:
            kw['accum_out'] = accum
            wr.append(accum)
        return self.add('act', lambda e: e.activation(out, in_, func, **kw), rd, wr)

    def tt(self, eng, out, in0, in1, op):
        return self.add(eng, lambda e: e.tensor_tensor(out, in0, in1, op), [in0, in1], [out])

    def ts(self, eng, out, in0, s1, s2, op0, op1=None):
        rd = [in0] + [s for s in (s1, s2) if s is not None and not isinstance(s, (int, float))]
        if op1 is None:
            return self.add(eng, lambda e: e.tensor_scalar(out, in0, s1, None, op0), rd, [out])
        return self.add(eng, lambda e: e.tensor_scalar(out, in0, s1, s2, op0, op1), rd, [out])

    def stt(self, eng, out, in0, scalar, in1, op0, op1):
        rd = [in0, in1] + ([] if isinstance(scalar, (int, float)) else [scalar])
        return self.add(eng, lambda e: e.scalar_tensor_tensor(out, in0, scalar, in1, op0, op1), rd, [out])

    def cp(self, eng, out, in_):
        if eng == 'act':
            return self.add('act', lambda e: e.copy(out, in_), [in_], [out])
        return self.add(eng, lambda e: e.tensor_copy(out, in_), [in_], [out])

    def red(self, eng, out, in_, op):
        return self.add(eng, lambda e: e.tensor_reduce(out, in_, AX.X, op), [in_], [out])

    def recip(self, out, in_):
        return self.add('dve', lambda e: e.reciprocal(out, in_), [in_], [out])

    def memset(self, eng, ap, val):
        return self.add(eng, lambda e: e.memset(ap, val), [], [ap])

    def dma(self, q, out, in_, final=False, nsem=8, after=()):
        i = self.add(q, lambda e: e.dma_start(out, in_), [in_] + list(after), [out], dma=True, nsem=nsem)
        if final:
            self.ops[i]['final'] = True
        return i


class Planner:
    def __init__(self, limit):
        self.items = []
        self.limit = limit

    def place(self, name, nbytes, phases):
        nbytes = (nbytes + CELL - 1) // CELL * CELL
        phases = set(phases)
        cands = sorted([(o, o + n) for (_, o, n, ph) in self.items if ph & phases])
        off = 0
        for lo, hi in cands:
            if off + nbytes <= lo:
                break
            off = max(off, hi)
        assert off + nbytes <= self.limit, f"SBUF overflow placing {name}: {off}+{nbytes}"
        self.items.append((name, off, nbytes, phases))
        return off


def _fc_layout():
    lay = {}
    off = 0
    for name, n in [('ssd_w', 512), ('qw', 64), ('kw', 64),
                    ('conv_w', 32), ('conv_b', 8), ('dtb', 8), ('alog', 8), ('dsk', 8),
                    ('cos', 512), ('sin', 512), ('U', 128), ('ones', 128)]:
        lay[name] = (off, n)
        off += n
    return lay, off


FC_LAY, FC_N = _fc_layout()
BC_N = 128 + 16 * 128


class _PlanDone(Exception):
    pass


_PLAN = {}


def build_nc():
    if not _PLAN:
        try:
            _build_nc(True)
        except _PlanDone:
            pass
    return _build_nc(False)


def _build_nc(collect):
    nc = bass.Bass("TRN2", target_bir_lowering=False)
    P = Prog(nc)

    x_d = nc.dram_tensor("x", [T, D], F32, kind="ExternalInput").ap()
    win_d = nc.dram_tensor("w_in", [D, 3080], F32, kind="ExternalInput").ap()
    wout_d = nc.dram_tensor("w_out", [D, D], F32, kind="ExternalInput").ap()
    wup_d = nc.dram_tensor("w_up", [D, 4096], F32, kind="ExternalInput").ap()
    wdn_d = nc.dram_tensor("w_down", [4096, D], F32, kind="ExternalInput").ap()
    fc_d = nc.dram_tensor("fconst", [128, FC_N], F32, kind="ExternalInput").ap()
    bc_d = nc.dram_tensor("bconst", [128, BC_N], BF16, kind="ExternalInput").ap()
    nw_d = nc.dram_tensor("normw", [128, 2048], F32, kind="ExternalInput").ap()
    out_d = nc.dram_tensor("out", [T, D], F32, kind="ExternalOutput").ap()

    plan = Planner(206 * 1024)
    reqs = []

    def plan_done():
        if not collect:
            return
        orders = [lambda r: (-r[1], r[0]), lambda r: (-len(r[2]), -r[1], r[0]),
                  lambda r: (min(r[2]), -r[1], r[0]), lambda r: (-len(r[2]) * r[1], r[0]),
                  lambda r: (-max(r[2]), -r[1], r[0])]
        err = None
        for key in orders:
            pl = Planner(plan.limit)
            try:
                tmp = {}
                for name, nb, ph in sorted(reqs, key=key):
                    tmp[name] = pl.place(name, nb, ph)
                _PLAN.update(tmp)
                err = None
                break
            except AssertionError as e_:
                err = e_
        if err is not None:
            raise err
        raise _PlanDone()

    ALLP = range(7)

    def sb(name, shape, dt, phases):
        n = 1
        for s in shape[1:]:
            n *= s
        if collect:
            reqs.append((name, n * _esz(dt), set(phases)))
            return None
        off = _PLAN[name] + 16512
        t = nc.alloc_sbuf_tensor_at(name, list(shape), dt, offset=off)
        a = t.ap()
        P.base[a.tensor.name] = ('S', off)
        return a

    banks = []
    bpair = []
    for i in range(4):
        t = nc.alloc_psum_tensor(f"pp{i}", [128, 1024], F32)
        a = t.ap()
        P.base[a.tensor.name] = ('P', i * 4096)
        bpair.append(a)
        banks.append(a[:, 0:512])
        banks.append(a[:, 512:1024])

    def bankbf(i):
        a = banks[i].bitcast(BF16)
        return a

    fc = sb("fc", [128, FC_N], F32, ALLP)
    bc = sb("bc", [128, BC_N], BF16, ALLP)
    stats = sb("stats", [128, 512], F32, ALLP)
    nstat = sb("nstat", [128, 4 * 256], F32, ALLP)

    hT = sb("hT", [128, 8, T], BF16, [0, 1, 2])
    nwA = sb("nwA", [128, D], F32, [0])
    nwM = sb("nwM", [128, D], F32, [5])
    NXB = 6
    xblk = [sb(f"xblk{i}", [128, D], F32, [0]) for i in range(NXB)]
    hn = [sb(f"hn{i}", [128, D], BF16, [0, 5]) for i in range(2)]
    junk = sb("junk", [128, D], BF16, [0, 5])
    WinX = sb("WinX", [128, 8, 1024], BF16, [0, 1])
    WinR = sb("WinR", [128, 8, 2056], BF16, [0, 1, 2])
    stage = [sb(f"stage{i}", [128, 2052], F32, [1]) for i in range(2)]
    cacc = [sb(f"cacc{i}", [128, 2048], F32, [1]) for i in range(2)]
    xbcT = sb("xbcT", [128, 8, T], BF16, [1, 2, 3])
    QKT = sb("QKT", [128, 8, T], BF16, [2, 3, 4])
    Vaug = sb("Vaug", [128, NB, 8, 65], BF16, [2, 3, 4])
    zs = sb("zs", [128, NB, 512], BF16, [2, 3])
    qk = [sb(f"qk{i}", [128, 1024], F32, [2]) for i in range(2)]
    tcb = sb("tcb", [128, 1024], F32, [2])
    tsb = sb("tsb", [128, 1024], F32, [2])
    rot = [sb(f"rot{i}", [128, 1024], BF16, [2]) for i in range(3)]
    b2s = sb("b2s", [128, 256], F32, [2])
    cwsw = sb("cwsw", [128, 2, 256], F32, [2])
    ztb = sb("ztb", [128, 512], F32, [2])
    qkw = sb("qkw", [128, 128], F32, [2])
    catT = sb("catT", [128, 8, T], BF16, [3, 4, 5])
    Wout = sb("Wout", [128, 8, D], BF16, [4, 5])
    xsB = [sb(f"xsB{i}", [128, 768], BF16, [3]) for i in range(2)]
    rhs_cs = [sb(f"rhs_cs{i}", [128, 1024], F32, [3]) for i in range(2)]
    dmat = [sb(f"dmat{i}", [128, 1024], F32, [3]) for i in range(2)]
    ebf = [sb(f"ebf{i}", [128, 1024], BF16, [3]) for i in range(2)]
    cbm = [sb(f"cbm{i}", [128, 256], BF16, [3]) for i in range(2)]
    Gm = [sb(f"Gm{i}", [128, 1024], BF16, [3]) for i in range(2)]
    c3s = [sb(f"c3s{i}", [128, 128], F32, [3]) for i in range(2)]
    xc = [sb(f"xc{i}", [128, 512], BF16, [3]) for i in range(2)]
    xcd = [sb(f"xcd{i}", [128, 512], BF16, [3]) for i in range(2)]
    xsd = [sb(f"xsd{i}", [128, 512], F32, [3]) for i in range(2)]
    yo_s = sb("yo_s", [128, 512], F32, [3])
    ybuf = sb("ybuf", [128, 512], F32, [3])
    t2 = sb("t2", [128, 512], F32, [3])
    yn2 = [sb(f"yn{i}", [128, 512], BF16, [3]) for i in range(2)]
    STf = sb("STf", [128, 512], F32, [3])
    STt = sb("STt", [128, 512], F32, [3])
    STb = sb("STb", [128, 512], BF16, [3])
    Pbp = [sb(f"Pbp{i}", [128, 1024], BF16, [4]) for i in range(2)]
    Pmp = [sb(f"Pmp{i}", [128, 1024], BF16, [4]) for i in range(2)]
    rden = [sb(f"rden{i}", [128, 4], F32, [4]) for i in range(2)]
    KTp = [sb(f"KTp{i}", [128, 4, T], BF16, [4]) for i in range(2)]
    o_tok = [sb(f"o_tok{i}", [128, 4, 512], BF16, [4]) for i in range(2)]
    Wm = [sb(f"Wm{i}", [128, 8, 1024], BF16, [4, 5, 6] if i < 2 else [6]) for i in range(3)]
    x1 = sb("x1", [128, NB, D], F32, [5, 6])
    hmT = sb("hmT", [128, 8, T], BF16, [5, 6])
    aT = sb("aT", [128, 8, T], BF16, [6])
    rbuf = [sb(f"rbuf{i}", [128, 512], F32, [6]) for i in range(3)]
    plan_done()

    QT = QKT[:, 0:4]
    KT = QKT[:, 4:8]

    def fcs(name):
        o, n = FC_LAY[name]
        return fc[:, o:o + n]

    ident = bc[:, 0:128]
    masks = bc[:, 128:128 + 2048]
    Uf = fcs('U')
    onesf = fcs('ones')
    cos_t = fcs('cos').rearrange("p (b f) -> p b f", f=32)
    sin_t = fcs('sin').rearrange("p (b f) -> p b f", f=32)
    convw = fcs('conv_w').rearrange("p (c t) -> p c t", t=4)
    convb = fcs('conv_b')

    ss = nstat[:, 0:256]
    rs = nstat[:, 256:512]
    ss2 = nstat[:, 512:768]
    rs2 = nstat[:, 768:1024]
    aneg = stats[:, 64:72]
    dt_all = stats[:, 128:256].rearrange("p (b h) -> p b h", h=8)
    adt_all = stats[:, 256:384].rearrange("p (b h) -> p b h", h=8)
    sm = stats[:, 384:512]


    win_v = win_d.rearrange("(kc p) n -> p kc n", p=128)
    wout_v = wout_d.rearrange("(kc p) n -> p kc n", p=128)
    wup_v = wup_d.rearrange("(kc p) n -> p kc n", p=128)
    wdn_v = wdn_d.rearrange("(f p) n -> p f n", p=128)

    P.dma('sp', xblk[0], x_d[0:128, :], nsem=12)
    P.dma('sp', nwA, nw_d[:, 0:1024], nsem=12)
    P.dma('sp', bc, bc_d, nsem=12)
    P.dma('sp', xblk[1], x_d[128:256, :], nsem=12)
    P.dma('sp', fc, fc_d, nsem=12)
    for b_ in range(2, NXB - 1):
        P.dma('sp', xblk[b_], x_d[b_ * 128:(b_ + 1) * 128, :], nsem=12)
    P.dma('pool', WinX, win_v[:, :, 2048:3072], after=[xblk[1]])
    P.memset('dve', ss, 0.0)
    P.memset('dve', ss2, 0.0)

    def rms_front(b, xin, wap, ssv, rsv, hnb):
        sc = slice(b * 16, b * 16 + 1)
        P.act(junk, xin, AF.Square, accum=ssv[:, sc])
        P.act(rsv[:, sc], ssv[:, sc], AF.Sqrt, bias=EPS, scale=1.0 / D)
        P.recip(rsv[:, sc], rsv[:, sc])
        P.stt('dve', hnb, xin, rsv[:, sc], wap, ALU.mult, ALU.mult)

    def rms_back(b, hnb, dstT, psbank):
        pt = bankbf(psbank)
        for kc in range(8):
            P.tr(pt[:, kc * 128:(kc + 1) * 128], hnb[:, kc * 128:(kc + 1) * 128], ident)
        P.cp('dve' if b % 2 == 0 else 'act', dstT[:, :, b * 128:(b + 1) * 128],
             pt.rearrange("p (k t) -> p k t", t=128))

    for b in range(NB):
        if b + NXB - 1 < NB:
            nb_ = b + NXB - 1
            P.dma('sp', xblk[nb_ % NXB], x_d[nb_ * 128:(nb_ + 1) * 128, :], nsem=12)
        rms_front(b, xblk[b % NXB], nwA, ss, rs, hn[b % 2])
        if b > 0:
            rms_back(b - 1, hn[(b - 1) % 2], hT, (b - 1) % 2)
    rms_back(NB - 1, hn[(NB - 1) % 2], hT, (NB - 1) % 2)
    for kc in range(0, 8, 4):
        P.dma('pool', WinR[:, kc:kc + 4, 0:2048], win_v[:, kc:kc + 4, 0:2048], after=[hT[:, 0, 1024:1152]])
    P.dma('pool', WinR[:, :, 2048:2056], win_v[:, :, 3072:3080])

    for i in range(2):
        P.memset('dve', stage[i][:, 0:3], 0.0)
    bk = 0
    for cc in range(8):
        stg = stage[cc % 2]
        acc = cacc[cc % 2]
        for g in range(4):
            ps = banks[2 + bk % 4]
            bk += 1
            for kc in range(8):
                P.mm(ps, WinX[:, kc, cc * 128:(cc + 1) * 128], hT[:, kc, g * 512:(g + 1) * 512],
                     start=(kc == 0), stop=(kc == 7))
            P.cp('act', stg[:, 3 + g * 512:3 + (g + 1) * 512], ps)
        P.act(acc, stg[:, 0:T], AF.Identity, bias=convb[:, cc:cc + 1], scale=convw[:, cc, 0:1])
        if cc > 0:
            P.act(xbcT[:, cc - 1, :], cacc[(cc - 1) % 2], AF.Silu)
        for tap in range(1, 4):
            P.stt('dve', acc, stg[:, tap:tap + T], convw[:, cc, tap:tap + 1], acc, ALU.mult, ALU.add)
    P.act(xbcT[:, 7, :], cacc[7 % 2], AF.Silu)

    P.act(aneg, fcs('alog'), AF.Exp)
    P.ts('dve', aneg, aneg, -1.0, None, ALU.mult)
    P.memset('pool', Vaug[:, :, :, 64:65], 1.0)
    P.ts('dve', qkw[:, 0:64], fcs('qw'), 0.125, None, ALU.mult)
    P.cp('dve', qkw[:, 64:128], fcs('kw'))

    def b2_front(b):
        blk = slice(b * 128, (b + 1) * 128)
        q_ = qk[b % 2]
        so = (b % 2) * 128
        zbank = banks[3] if b % 2 == 0 else banks[7]
        psdt = banks[4][:, 0:8]
        for kc in range(8):
            P.mm(psdt, hT[:, kc, blk], WinR[:, kc, 2048:2056], start=(kc == 0), stop=(kc == 7))
        for gi in range(4):
            ps = zbank if gi == 3 else banks[gi]
            for kc in range(8):
                P.mm(ps, hT[:, kc, blk], WinR[:, kc, gi * 512:(gi + 1) * 512], start=(kc == 0), stop=(kc == 7))
        xb_ = b2s[:, so + 0:so + 8]
        ax_ = b2s[:, so + 8:so + 16]
        ex_ = b2s[:, so + 16:so + 24]
        P.tt('dve', xb_, psdt, fcs('dtb'), ALU.add)
        P.stt('dve', ax_, xb_, -1.0, xb_, ALU.mult, ALU.max)
        P.cp('act', q_[:, 0:512], banks[0])
        P.cp('act', q_[:, 512:1024], banks[1])
        ssq = b2s[:, so + 32:so + 48]
        rq = b2s[:, so + 48:so + 64]
        P.act(tsb, q_, AF.Square)
        P.red('dve', ssq, tsb.rearrange("p (h d) -> p h d", d=64), ALU.add)
        P.cp('act', Vaug[:, b, :, 0:64], banks[2].rearrange("p (h d) -> p h d", d=64))
        P.act(ex_, ax_, AF.Exp, scale=-1.0)
        P.act(ex_, ex_, AF.Ln, bias=1.0)
        P.act(rq, ssq, AF.Ln, bias=EPS, scale=1.0 / 64)
        P.act(rq, rq, AF.Exp, scale=-0.5)
        P.act(ztb, zbank, AF.Exp, scale=-1.0)
        P.act(ztb, ztb, AF.Ln, bias=1.0)
        P.act(ztb, ztb, AF.Exp, scale=-1.0)
        P.stt('dve', dt_all[:, b, :], xb_, 0.0, ex_, ALU.max, ALU.add)
        P.tt('dve', adt_all[:, b, :], dt_all[:, b, :], aneg, ALU.mult)
        q3 = q_.rearrange("p (h d) -> p h d", d=64)
        P.tt('dve', q3, q3, rq.unsqueeze(2).to_broadcast([128, 16, 64]), ALU.mult)
        cw_ = cwsw[:, b % 2, 0:128]
        sw_ = cwsw[:, b % 2, 128:256]
        g4 = qkw.rearrange("p (w t f) -> p w t f", w=2, t=2)
        P.tt('pool', cw_.rearrange("p (w t f) -> p w t f", w=2, t=2), g4,
             cos_t[:, b, :].unsqueeze(1).unsqueeze(1).to_broadcast([128, 2, 2, 32]), ALU.mult)
        P.tt('pool', sw_.rearrange("p (w t f) -> p w t f", w=2, t=2), g4,
             sin_t[:, b, :].unsqueeze(1).unsqueeze(1).to_broadcast([128, 2, 2, 32]), ALU.mult)
        qv = q_.rearrange("p (w h d) -> p w h d", w=2, d=64)
        tcv = tcb.rearrange("p (w h d) -> p w h d", w=2, d=64)
        tsv = tsb.rearrange("p (w h d) -> p w h d", w=2, d=64)
        P.tt('dve', tcv, qv, cw_.rearrange("p (w d) -> p w d", d=64).unsqueeze(2).to_broadcast([128, 2, 8, 64]),
             ALU.mult)
        P.tt('pool', tsv, qv, sw_.rearrange("p (w d) -> p w d", d=64).unsqueeze(2).to_broadcast([128, 2, 8, 64]),
             ALU.mult)
        tc4 = tcb.rearrange("p (h t f) -> p h t f", t=2, f=32)
        ts4 = tsb.rearrange("p (h t f) -> p h t f", t=2, f=32)
        r4 = rot[b % 3].rearrange("p (h t f) -> p h t f", t=2, f=32)
        P.tt('dve', r4[:, :, 0, :], tc4[:, :, 0, :], ts4[:, :, 1, :], ALU.subtract)
        P.tt('dve', r4[:, :, 1, :], tc4[:, :, 1, :], ts4[:, :, 0, :], ALU.add)
        P.tt('dve', zs[:, b, :], zbank, ztb, ALU.mult)

    def b2_back(b):
        blk = slice(b * 128, (b + 1) * 128)
        pt = bankbf(5 + b % 2)
        for pr in range(8):
            P.tr(pt[:, pr * 128:(pr + 1) * 128], rot[b % 3][:, pr * 128:(pr + 1) * 128], ident)
        P.cp('act', QKT[:, :, blk], pt.rearrange("p (k t) -> p k t", t=128))

    for b in range(NB):
        b2_front(b)
        if b > 1:
            b2_back(b - 2)
    b2_back(NB - 2)
    b2_back(NB - 1)


    P.memset('dve', STf, 0.0)
    P.memset('pool', STb, 0.0)
    dskb = fcs('dsk')

    def ssd_f0(c):
        p = c % 2
        blk = slice(c * 128, (c + 1) * 128)
        P.tt('pool', rhs_cs[p].rearrange("p (h l) -> p h l", l=128),
             Uf.unsqueeze(1).to_broadcast([128, 8, 128]),
             adt_all[:, c, :].unsqueeze(2).to_broadcast([128, 8, 128]), ALU.mult)
        for j in range(2):
            P.mm(banks[1 + j], onesf, rhs_cs[p][:, j * 512:(j + 1) * 512])
        P.mm(banks[3][:, 0:8], Uf, adt_all[:, c, :])
        for g in range(2):
            P.mm(banks[3][:, 128 + g * 128:128 + (g + 1) * 128], xbcT[:, 4 + g, blk], xbcT[:, 6 + g, blk])

    def ssd_f1(c):
        p = c % 2
        blk = slice(c * 128, (c + 1) * 128)
        ptin = bankbf(0)
        for cc in range(4):
            P.tr(ptin[:, cc * 128:(cc + 1) * 128], xbcT[:, cc, blk], ident)
        for g in range(2):
            P.tr(ptin[:, 512 + g * 128:512 + (g + 1) * 128], xbcT[:, 4 + g, blk], ident)
        P.cp('act', xsB[p], ptin[:, 0:768])
        acs_t = c3s[p][:, 64:72]
        dte = c3s[p][:, 0:8]
        dec = c3s[p][:, 16:24]
        ea = c3s[p][:, 32:40]
        P.cp('dve', acs_t, banks[3][:, 0:8])
        for h in range(8):
            P.act(dmat[p][:, h * 128:(h + 1) * 128], banks[1 + h // 4][:, (h % 4) * 128:(h % 4 + 1) * 128],
                  AF.Relu, bias=acs_t[:, h:h + 1], scale=-1.0)
        P.act(ebf[p], dmat[p], AF.Exp, scale=-1.0)
        for j in range(2):
            last = banks[1 + j].rearrange("p (h l) -> p h l", l=128)[:, :, 127]
            P.tt('dve', dte[:, j * 4:(j + 1) * 4], last, acs_t[:, j * 4:(j + 1) * 4], ALU.subtract)
            P.act(dec[:, j * 4:(j + 1) * 4], last, AF.Exp)
        P.act(dte, dte, AF.Exp)
        P.act(ea, acs_t, AF.Exp)
        P.tt('dve', cbm[p].rearrange("p (g l) -> p g l", l=128),
             banks[3][:, 128:384].rearrange("p (g l) -> p g l", l=128),
             Uf.unsqueeze(1).to_broadcast([128, 2, 128]), ALU.mult)

    def ssd_f2(c):
        p = c % 2
        xs_tok = xsB[p][:, 0:512]
        dte = c3s[p][:, 0:8]
        P.tt('dve', Gm[p].rearrange("p (g r l) -> p g r l", r=4, l=128),
             ebf[p].rearrange("p (g r l) -> p g r l", r=4, l=128),
             cbm[p].rearrange("p (g l) -> p g l", l=128).unsqueeze(2).to_broadcast([128, 2, 4, 128]), ALU.mult)
        P.tt('pool', xc[p].rearrange("p (h d) -> p h d", d=64), xs_tok.rearrange("p (h d) -> p h d", d=64),
             dt_all[:, c, :].unsqueeze(2).to_broadcast([128, 8, 64]), ALU.mult)
        P.tt('pool', xcd[p].rearrange("p (h d) -> p h d", d=64), xc[p].rearrange("p (h d) -> p h d", d=64),
             dte.unsqueeze(2).to_broadcast([128, 8, 64]), ALU.mult)
        P.tt('pool', xsd[p].rearrange("p (h d) -> p h d", d=64), xs_tok.rearrange("p (h d) -> p h d", d=64),
             dskb.unsqueeze(2).to_broadcast([128, 8, 64]), ALU.mult)

    def ssd_f3(c):
        p = c % 2
        for h in range(8):
            P.mm(banks[4][:, h * 64:(h + 1) * 64], Gm[p][:, h * 128:(h + 1) * 128], xc[p][:, h * 64:(h + 1) * 64])
        for g in range(2):
            P.mm(banks[6][:, g * 256:(g + 1) * 256], xsB[p][:, 512 + g * 128:512 + (g + 1) * 128],
                 xcd[p][:, g * 256:(g + 1) * 256])

    def ssd_b0(c):
        blk = slice(c * 128, (c + 1) * 128)
        for g in range(2):
            P.mm(banks[5][:, g * 256:(g + 1) * 256], xbcT[:, 6 + g, blk], STb[:, g * 256:(g + 1) * 256])

    def ssd_b1(c):
        p = c % 2
        blk = slice(c * 128, (c + 1) * 128)
        dec = c3s[p][:, 16:24]
        ea = c3s[p][:, 32:40]
        P.tt('dve', yo_s.rearrange("p (h d) -> p h d", d=64), banks[5].rearrange("p (h d) -> p h d", d=64),
             ea.unsqueeze(2).to_broadcast([128, 8, 64]), ALU.mult)
        if c + 1 < NB:
            P.tt('dve', STt.rearrange("p (h d) -> p h d", d=64), STf.rearrange("p (h d) -> p h d", d=64),
                 dec.unsqueeze(2).to_broadcast([128, 8, 64]), ALU.mult)
            P.tt('dve', STf, STt, banks[6], ALU.add)
            P.cp('act', STb, STf)
        P.tt('dve', ybuf, yo_s, banks[4], ALU.add)
        P.tt('dve', ybuf, ybuf, xsd[p], ALU.add)
        P.tt('dve', ybuf, ybuf, zs[:, c, :], ALU.mult)
        ssy = c3s[p][:, 48:50]
        ry = c3s[p][:, 52:54]
        P.memset('dve', ssy, 0.0)
        for g in range(2):
            P.act(t2[:, g * 256:(g + 1) * 256], ybuf[:, g * 256:(g + 1) * 256], AF.Square, accum=ssy[:, g:g + 1])
        P.act(ry, ssy, AF.Ln, bias=EPS, scale=1.0 / 256)
        P.act(ry, ry, AF.Exp, scale=-0.5)

    def ssd_b2(c):
        p = c % 2
        yn = yn2[p]
        blk = slice(c * 128, (c + 1) * 128)
        ry = c3s[p][:, 52:54]
        for g in range(2):
            P.stt('dve', yn[:, g * 256:(g + 1) * 256], ybuf[:, g * 256:(g + 1) * 256], ry[:, g:g + 1],
                  fcs('ssd_w')[:, g * 256:(g + 1) * 256], ALU.mult, ALU.mult)
        ptout = bankbf(7)
        for j in range(4):
            P.tr(ptout[:, j * 128:(j + 1) * 128], yn[:, j * 128:(j + 1) * 128], ident)
        P.cp('act', catT[:, 4:8, blk], ptout[:, 0:512].rearrange("p (k t) -> p k t", t=128))

    ssd_f0(0)
    for c in range(NB):
        if c > 0:
            ssd_b0(c - 1)
        ssd_f1(c)
        if c + 1 < NB:
            ssd_f0(c + 1)
        if c > 0:
            ssd_b1(c - 1)
        ssd_f2(c)
        if c > 0:
            ssd_b2(c - 1)
        ssd_f3(c)
    ssd_b0(NB - 1)
    ssd_b1(NB - 1)
    ssd_b2(NB - 1)

    mlp_items = []
    for fg in range(4):
        mlp_items.append(('u', fg))
        mlp_items.append(('d', fg))

    def load_mlp(i):
        kind, fg = mlp_items[i]
        dst = Wm[i % 3]
        src = wup_v[:, :, fg * 1024:(fg + 1) * 1024] if kind == 'u' else wdn_v[:, fg * 8:(fg + 1) * 8, :]
        for k in range(0, 8, 4):
            P.dma('pool', dst[:, k:k + 4, :], src[:, k:k + 4, :])

    units = []
    for g in range(4):
        for h in range(8):
            for kb in range(4 * g + 4):
                units.append((g, h, kb))
    groups = []
    i_ = 0
    while i_ < len(units):
        g1, h1, kb1 = units[i_]
        if i_ + 1 < len(units):
            g2, h2, kb2 = units[i_ + 1]
            if max(0, kb2 - 4 * g2) == 0:
                groups.append([i_, i_ + 1])
                i_ += 2
                continue
        groups.append([i_])
        i_ += 1
    slot_of = {}
    for j, grp in enumerate(groups):
        for half, u in enumerate(grp):
            slot_of[u] = (j % 2, half)

    def att_scores_group(j):
        grp = groups[j]
        ps_pair = bpair[j % 2]
        c_first = None
        for u in grp:
            g, h, kb = units[u]
            pr, hp = h // 2, h % 2
            r = max(0, kb - 4 * g)
            sp_, half = slot_of[u]
            c0 = half * 512 + r * 128
            if c_first is None:
                c_first = c0
            c_end = half * 512 + 512
            P.mm(ps_pair[:, c0:c_end], KTp[hp][:, pr, kb * 128:(kb + 1) * 128],
                 QT[:, pr, g * 512 + r * 128:(g + 1) * 512])
        P.act(Pbp[j % 2][:, c_first:c_end], ps_pair[:, c_first:c_end], AF.Exp)
        for u in grp:
            g, h, kb = units[u]
            r = max(0, kb - 4 * g)
            sp_, half = slot_of[u]
            c0 = half * 512 + r * 128
            c_end = half * 512 + 512
            d0 = 4 * g + r - kb
            nj = 4 - r
            P.tt('dve', Pmp[j % 2][:, c0:c_end], Pbp[j % 2][:, c0:c_end],
                 masks[:, d0 * 128:(d0 + nj) * 128], ALU.mult)

    def att_pv(i):
        g, h, kb = units[i]
        gi = g * 8 + h
        acc3 = banks[4 + gi % 2][:, 0:260].rearrange("p (j e) -> p j e", e=65)
        r = max(0, kb - 4 * g)
        sp_, half = slot_of[i]
        pm = Pmp[sp_][:, half * 512:(half + 1) * 512]
        for jj in range(r, 4):
            P.mm(acc3[:, jj, :], pm[:, jj * 128:(jj + 1) * 128], Vaug[:, kb, h, :],
                 start=(kb == 0 and jj == 0), stop=(kb == 4 * g + jj), skip=True)
        if kb == 4 * g + 3:
            rd = rden[gi % 2]
            ot = o_tok[g % 2]
            P.recip(rd, acc3[:, :, 64])
            P.tt('dve', ot[:, :, h * 64:(h + 1) * 64], acc3[:, :, 0:64],
                 rd.unsqueeze(2).to_broadcast([128, 4, 64]), ALU.mult)
            if h == 7:
                for jj in range(4):
                    b = 4 * g + jj
                    pt = bankbf(6 + b % 2)
                    for j in range(4):
                        P.tr(pt[:, j * 128:(j + 1) * 128], ot[:, jj, j * 128:(j + 1) * 128], ident)
                    P.cp('dve', catT[:, 0:4, b * 128:(b + 1) * 128],
                         pt[:, 0:512].rearrange("p (k t) -> p k t", t=128))
                if g == 0:
                    load_mlp(0)
                    load_mlp(1)

    P.dma('pool', Wout, wout_v)
    P.add('act', lambda e: e.memzero(KTp[0][64:128]), [], [KTp[0][64:128]])
    P.add('act', lambda e: e.memzero(KTp[1][0:64]), [], [KTp[1][0:64]])
    for pr_ in range(4):
        P.cp('dve', KTp[0][0:64, pr_, :], KT[0:64, pr_, :])
        P.cp('dve' if pr_ < 2 else 'act', KTp[1][64:128, pr_, :], KT[64:128, pr_, :])
    att_scores_group(0)
    for j in range(len(groups)):
        if j + 1 < len(groups):
            att_scores_group(j + 1)
        for u in groups[j]:
            att_pv(u)

    P.dma('sp', nwM, nw_d[:, 1024:2048])
    for b in range(NB):
        P.dma('sp', x1[:, b, :], x_d[b * 128:(b + 1) * 128, :])
    for b in range(NB):
        blk = slice(b * 128, (b + 1) * 128)
        for n in range(2):
            ps = banks[(b % 2) * 2 + n]
            for kc in range(8):
                P.mm(ps, catT[:, kc, blk], Wout[:, kc, n * 512:(n + 1) * 512], start=(kc == 0), stop=(kc == 7))
            P.tt('dve', x1[:, b, n * 512:(n + 1) * 512], x1[:, b, n * 512:(n + 1) * 512], ps, ALU.add)
        rms_front(b, x1[:, b, :], nwM, ss2, rs2, hn[b % 2])
        if b > 0:
            rms_back(b - 1, hn[(b - 1) % 2], hmT, 4 + (b - 1) % 2)
    rms_back(NB - 1, hn[(NB - 1) % 2], hmT, 4 + (NB - 1) % 2)

    ui = 0
    ri = 0
    for fg in range(4):
        if 2 * fg + 2 < 8:
            load_mlp(2 * fg + 2)
        Wu = Wm[(2 * fg) % 3]
        Wd = Wm[(2 * fg + 1) % 3]
        for tg in range(4):
            for fj in range(8):
                ps = banks[ui % 4]
                ui += 1
                for kc in range(8):
                    P.mm(ps, Wu[:, kc, fj * 128:(fj + 1) * 128], hmT[:, kc, tg * 512:(tg + 1) * 512],
                         start=(kc == 0), stop=(kc == 7))
                rb = rbuf[ri % 3]
                ri += 1
                P.act(rb, ps, AF.Relu)
                P.tt('dve', aT[:, fj, tg * 512:(tg + 1) * 512], rb, rb, ALU.mult)
        if 2 * fg + 3 < 8:
            load_mlp(2 * fg + 3)
        for b in range(NB):
            blk = slice(b * 128, (b + 1) * 128)
            for n in range(2):
                ps = banks[4 + (b % 2) * 2 + n]
                for fj in range(8):
                    P.mm(ps, aT[:, fj, blk], Wd[:, fj, n * 512:(n + 1) * 512], start=(fj == 0), stop=(fj == 7))
                P.tt('dve', x1[:, b, n * 512:(n + 1) * 512], x1[:, b, n * 512:(n + 1) * 512], ps, ALU.add)
            if fg == 3:
                P.dma('sp', out_d[b * 128:(b + 1) * 128, :], x1[:, b, :], final=True)

    P.emit()
    return nc


def _consts():
    t = np.arange(T, dtype=np.float64)
    half = 32
    inv = 10000.0 ** (-np.arange(half, dtype=np.float64) / half)
    ang = t[:, None] * inv[None, :]
    cos = np.cos(ang).astype(np.float32).reshape(NB, 128, 32).transpose(1, 0, 2).reshape(128, 512)
    sin = np.sin(ang).astype(np.float32).reshape(NB, 128, 32).transpose(1, 0, 2).reshape(128, 512)
    U = np.triu(np.ones((128, 128), np.float32))
    ones = np.ones((128, 128), np.float32)
    i = np.arange(128)
    masks = np.zeros((128, 16, 128), np.float32)
    for d in range(16):
        dl = 128 * d + i[None, :] - i[:, None]
        m = ((dl >= 0) & (dl <= 128)).astype(np.float32)
        m += ((dl >= 0) & (dl % 4 == 0) & (dl <= 512))
        m += ((dl >= 0) & (dl % 16 == 0) & (dl <= 2048))
        masks[:, d, :] = m
    ident = np.eye(128, dtype=np.float32)
    bcn = np.concatenate([ident, masks.reshape(128, 2048)], axis=1).astype(ml_dtypes.bfloat16)
    return cos, sin, U, ones, bcn


_CACHE = {}


def kernel(x, attn_norm_w, w_in, q_norm_w, k_norm_w, conv_w, conv_b, dt_bias, a_log, d_skip,
           ssd_norm_w, w_out, mlp_norm_w, w_up, w_down):
    f32 = np.float32
    cos, sin, U, ones, bcn = _consts()

    def bc(v):
        v = np.asarray(v, f32).reshape(1, -1)
        return np.broadcast_to(v, (128, v.shape[1]))

    cw = np.asarray(conv_w, f32)[0]
    cwp = cw.reshape(4, 8, 128).transpose(2, 1, 0).reshape(128, 32)
    cbp = np.asarray(conv_b, f32)[0].reshape(8, 128).T
    parts = {'ssd_w': bc(ssd_norm_w[0]),
             'qw': bc(q_norm_w[0]), 'kw': bc(k_norm_w[0]), 'conv_w': cwp, 'conv_b': cbp,
             'dtb': bc(dt_bias[0]), 'alog': bc(a_log[0]), 'dsk': bc(d_skip[0]),
             'cos': cos, 'sin': sin, 'U': U, 'ones': ones}
    fconst = np.zeros((128, FC_N), f32)
    for k, (o, n) in FC_LAY.items():
        fconst[:, o:o + n] = parts[k]
    if 'nc' not in _CACHE:
        _CACHE['nc'] = build_nc()
    nc = _CACHE['nc']
    x = np.asarray(x, f32)
    shared = {"w_in": np.ascontiguousarray(np.asarray(w_in, f32)[0]),
              "w_out": np.ascontiguousarray(np.asarray(w_out, f32)[0]),
              "w_up": np.ascontiguousarray(np.asarray(w_up, f32)[0]),
              "w_down": np.ascontiguousarray(np.asarray(w_down, f32)[0]),
              "fconst": fconst, "bconst": bcn,
              "normw": np.ascontiguousarray(np.concatenate([bc(attn_norm_w[0]), bc(mlp_norm_w[0])], axis=1))}
    in_maps = [dict(shared, x=np.ascontiguousarray(x[i])) for i in range(8)]
    res = run_bass_kernel_spmd(nc, in_maps, core_ids=list(range(8)))
    return np.stack([np.asarray(r["out"], f32) for r in res.results], axis=0)
```

```python
import numpy as np
import ml_dtypes
import concourse.bass as bass
import concourse.mybir as mybir
from concourse.bass_utils import run_bass_kernel_spmd

F32 = mybir.dt.float32
BF16 = mybir.dt.bfloat16
AF = mybir.ActivationFunctionType
ALU = mybir.AluOpType
AX = mybir.AxisListType

T = 2048
NB = 16
D = 1024
EPS = 1e-6
CELL = 64


def _esz(dt):
    return 2 if dt == BF16 else 4


class Prog:
    def __init__(self, nc):
        self.nc = nc
        self.ops = []
        self.cw = {}
        self.cr = {}
        self.base = {}
        self.last_on = {}
        self.waited = {}
        self.dma_ring = {}
        self.dma_cnt = {}
        self.dma_last = {}

    def rng(self, ap):
        name = ap.tensor.name
        if name not in self.base:
            return None
        space, base = self.base[name]
        esz = _esz(ap.dtype)
        a = ap.ap
        pstride = a[0][0]
        off = ap.offset % pstride if pstride > 0 else ap.offset
        ext = 0
        for st, cnt in a[1:]:
            ext += (cnt - 1) * abs(st)
        lo = base + off * esz
        hi = base + (off + ext + 1) * esz
        if space == 'P':
            return (space, lo // 2048, (hi - 1) // 2048)
        return (space, lo // CELL, (hi - 1) // CELL)

    def add(self, eng, fn, reads, writes, dma=False, nsem=8):
        idx = len(self.ops)
        deps = set()
        rr = [self.rng(a) for a in reads]
        ww = [self.rng(a) for a in writes]
        for r in rr:
            if r is None:
                continue
            sp, lo, hi = r
            for c in range(lo, hi + 1):
                w = self.cw.get((sp, c))
                if w is not None:
                    deps.add(w)
        for r in ww:
            if r is None:
                continue
            sp, lo, hi = r
            for c in range(lo, hi + 1):
                k = (sp, c)
                w = self.cw.get(k)
                if w is not None:
                    deps.add(w)
                rd = self.cr.get(k)
                if rd:
                    deps.update(rd.values())
        rkey = ('d', idx) if dma else eng
        for r in rr:
            if r is None:
                continue
            sp, lo, hi = r
            for c in range(lo, hi + 1):
                self.cr.setdefault((sp, c), {})[rkey] = idx
        for r in ww:
            if r is None:
                continue
            sp, lo, hi = r
            for c in range(lo, hi + 1):
                self.cw[(sp, c)] = idx
                self.cr[(sp, c)] = {}
        waits = []
        best = {}
        for d in deps:
            o = self.ops[d]
            if o['dma']:
                waits.append(d)
                continue
            pe = o['eng']
            if pe == eng and not dma:
                if eng == 'pe':
                    continue
            if d > best.get(pe, -1):
                best[pe] = d
        for pe, d in best.items():
            if self.waited.get((eng, pe), -1) >= d:
                continue
            self.waited[(eng, pe)] = d
            waits.append(d)
        op = dict(eng=eng, fn=fn, waits=waits, dma=dma, sig=False, idx=idx,
                  wcells=ww, rcells=rr)
        if dma:
            ring = self.dma_cnt.setdefault(eng, 0)
            slot = ring % nsem
            self.dma_cnt[eng] = ring + 1
            key = (eng, slot)
            prev = self.dma_last.get(key)
            op['slot'] = key
            op['prev'] = prev
            self.dma_last[key] = idx
            op['sig'] = True
        for d in waits:
            self.ops[d]['sig'] = True
        self.ops.append(op)
        return idx

    def _raw_or_waw(self, prod, rr, ww):
        pw = prod['wcells']
        for r in list(rr) + list(ww):
            if r is None:
                continue
            for q in pw:
                if q is None:
                    continue
                if q[0] == r[0] and not (q[2] < r[1] or r[2] < q[1]):
                    return True
        return False

    def emit(self, final_wait_eng='sp'):
        nc = self.nc
        engs = ['pe', 'act', 'dve', 'pool', 'sp']
        per = {e: [o for o in self.ops if o['eng'] == e] for e in engs}
        semname = {}
        cnt = {}
        for o in self.ops:
            if not o['sig']:
                continue
            if o['dma']:
                k = ('dma',) + o['slot']
                cnt[k] = cnt.get(k, 0) + 16
            else:
                k = ('c', o['eng'])
                cnt[k] = cnt.get(k, 0) + 1
            o['sk'] = k
            o['sv'] = cnt[k]
        keys = sorted(set(o['sk'] for o in self.ops if o['sig']), key=str)
        import contextlib
        with contextlib.ExitStack() as st:
            sems = {k: st.enter_context(nc.semaphore("s_" + "_".join(str(x) for x in k))) for k in keys}
            block = st.enter_context(nc.Block())
            final_dmas = [o for o in self.ops if o['dma'] and o.get('final')]

            def run(e, eng):
                wv = {}
                for o in per[e]:
                    wl = list(o['waits'])
                    if o['dma'] and o['prev'] is not None:
                        wl.append(o['prev'])
                    need = []
                    for d in wl:
                        p = self.ops[d]
                        k, v = p['sk'], p['sv']
                        if wv.get(k, 0) >= v:
                            continue
                        wv[k] = v
                        need.append((k, v))
                    fold = (e in ('act', 'dve', 'pool')) and (not o['dma']) and len(need) > 0
                    for (k, v) in (need[:-1] if fold else need):
                        eng.wait_ge(sems[k], v)
                    ins = o['fn'](eng)
                    if fold:
                        k, v = need[-1]
                        ins._wait_ge(sems[k], v)
                    if o['sig']:
                        ins.then_inc(sems[o['sk']], 16 if o['dma'] else 1)
                if e == final_wait_eng:
                    for o in final_dmas:
                        k, v = o['sk'], o['sv']
                        if wv.get(k, 0) >= v:
                            continue
                        wv[k] = v
                        eng.wait_ge(sems[k], v)

            if per['pe']:
                block.tensor(lambda eng: run('pe', eng))
            if per['act']:
                block.scalar(lambda eng: run('act', eng))
            if per['dve']:
                block.vector(lambda eng: run('dve', eng))
            if per['pool']:
                block.gpsimd(lambda eng: run('pool', eng))
            block.sync(lambda eng: run('sp', eng))

    def mm(self, out, lhsT, rhs, start=True, stop=True, skip=False):
        return self.add('pe', lambda e: e.matmul(out, lhsT, rhs, start=start, stop=stop, skip_group_check=skip),
                        [lhsT, rhs] + ([] if start else [out]), [out])

    def tr(self, out, in_, ident):
        return self.add('pe', lambda e: e.transpose(out, in_, ident), [in_, ident], [out])

    def act(self, out, in_, func, bias=None, scale=None, accum=None):
        kw = {}
        rd = [in_]
        if bias is not None:
            kw['bias'] = bias
            if not isinstance(bias, (int, float)):
                rd.append(bias)
        if scale is not None:
            kw['scale'] = scale
            if not isinstance(scale, (int, float)):
                rd.append(scale)
        wr = [out]
        if accum is not None:
            kw['accum_out'] = accum
            wr.append(accum)
        return self.add('act', lambda e: e.activation(out, in_, func, **kw), rd, wr)

    def tt(self, eng, out, in0, in1, op):
        return self.add(eng, lambda e: e.tensor_tensor(out, in0, in1, op), [in0, in1], [out])

    def ts(self, eng, out, in0, s1, s2, op0, op1=None):
        rd = [in0] + [s for s in (s1, s2) if s is not None and not isinstance(s, (int, float))]
        if op1 is None:
            return self.add(eng, lambda e: e.tensor_scalar(out, in0, s1, None, op0), rd, [out])
        return self.add(eng, lambda e: e.tensor_scalar(out, in0, s1, s2, op0, op1), rd, [out])

    def stt(self, eng, out, in0, scalar, in1, op0, op1):
        rd = [in0, in1] + ([] if isinstance(scalar, (int, float)) else [scalar])
        return self.add(eng, lambda e: e.scalar_tensor_tensor(out, in0, scalar, in1, op0, op1), rd, [out])

    def cp(self, eng, out, in_):
        if eng == 'act':
            return self.add('act', lambda e: e.copy(out, in_), [in_], [out])
        return self.add(eng, lambda e: e.tensor_copy(out, in_), [in_], [out])

    def red(self, eng, out, in_, op):
        return self.add(eng, lambda e: e.tensor_reduce(out, in_, AX.X, op), [in_], [out])

    def recip(self, out, in_):
        return self.add('dve', lambda e: e.reciprocal(out, in_), [in_], [out])

    def memset(self, eng, ap, val):
        return self.add(eng, lambda e: e.memset(ap, val), [], [ap])

    def dma(self, q, out, in_, final=False, nsem=8, after=()):
        i = self.add(q, lambda e: e.dma_start(out, in_), [in_] + list(after), [out], dma=True, nsem=nsem)
        if final:
            self.ops[i]['final'] = True
        return i


class Planner:
    def __init__(self, limit):
        self.items = []
        self.limit = limit

    def place(self, name, nbytes, phases):
        nbytes = (nbytes + CELL - 1) // CELL * CELL
        phases = set(phases)
        cands = sorted([(o, o + n) for (_, o, n, ph) in self.items if ph & phases])
        off = 0
        for lo, hi in cands:
            if off + nbytes <= lo:
                break
            off = max(off, hi)
        assert off + nbytes <= self.limit, f"SBUF overflow placing {name}: {off}+{nbytes}"
        self.items.append((name, off, nbytes, phases))
        return off


def _fc_layout():
    lay = {}
    off = 0
    for name, n in [('ssd_w', 512), ('qw', 64), ('kw', 64),
                    ('conv_w', 32), ('conv_b', 8), ('dtb', 8), ('alog', 8), ('dsk', 8),
                    ('cos', 512), ('sin', 512), ('U', 128), ('ones', 128)]:
        lay[name] = (off, n)
        off += n
    return lay, off


FC_LAY, FC_N = _fc_layout()
BC_N = 128 + 16 * 128


class _PlanDone(Exception):
    pass


_PLAN = {}


def build_nc():
    if not _PLAN:
        try:
            _build_nc(True)
        except _PlanDone:
            pass
    return _build_nc(False)


def _build_nc(collect):
    nc = bass.Bass("TRN2", target_bir_lowering=False)
    P = Prog(nc)

    x_d = nc.dram_tensor("x", [T, D], F32, kind="ExternalInput").ap()
    win_d = nc.dram_tensor("w_in", [D, 3080], F32, kind="ExternalInput").ap()
    wout_d = nc.dram_tensor("w_out", [D, D], F32, kind="ExternalInput").ap()
    wup_d = nc.dram_tensor("w_up", [D, 4096], F32, kind="ExternalInput").ap()
    wdn_d = nc.dram_tensor("w_down", [4096, D], F32, kind="ExternalInput").ap()
    fc_d = nc.dram_tensor("fconst", [128, FC_N], F32, kind="ExternalInput").ap()
    bc_d = nc.dram_tensor("bconst", [128, BC_N], BF16, kind="ExternalInput").ap()
    nw_d = nc.dram_tensor("normw", [128, 2048], F32, kind="ExternalInput").ap()
    out_d = nc.dram_tensor("out", [T, D], F32, kind="ExternalOutput").ap()

    plan = Planner(206 * 1024)
    reqs = []

    def plan_done():
        if not collect:
            return
        orders = [lambda r: (-r[1], r[0]), lambda r: (-len(r[2]), -r[1], r[0]),
                  lambda r: (min(r[2]), -r[1], r[0]), lambda r: (-len(r[2]) * r[1], r[0]),
                  lambda r: (-max(r[2]), -r[1], r[0])]
        err = None
        for key in orders:
            pl = Planner(plan.limit)
            try:
                tmp = {}
                for name, nb, ph in sorted(reqs, key=key):
                    tmp[name] = pl.place(name, nb, ph)
                _PLAN.update(tmp)
                err = None
                break
            except AssertionError as e_:
                err = e_
        if err is not None:
            raise err
        raise _PlanDone()

    ALLP = range(7)

    def sb(name, shape, dt, phases):
        n = 1
        for s in shape[1:]:
            n *= s
        if collect:
            reqs.append((name, n * _esz(dt), set(phases)))
            return None
        off = _PLAN[name] + 16512
        t = nc.alloc_sbuf_tensor_at(name, list(shape), dt, offset=off)
        a = t.ap()
        P.base[a.tensor.name] = ('S', off)
        return a

    banks = []
    for i in range(8):
        t = nc.alloc_psum_tensor(f"ps{i}", [128, 512], F32)
        a = t.ap()
        P.base[a.tensor.name] = ('P', i * 2048)
        banks.append(a)

    def bankbf(i):
        a = banks[i].bitcast(BF16)
        return a

    fc = sb("fc", [128, FC_N], F32, ALLP)
    bc = sb("bc", [128, BC_N], BF16, ALLP)
    stats = sb("stats", [128, 512], F32, ALLP)
    nstat = sb("nstat", [128, 4 * 256], F32, ALLP)

    hT = sb("hT", [128, 8, T], BF16, [0, 1, 2])
    nwA = sb("nwA", [128, D], F32, [0])
    nwM = sb("nwM", [128, D], F32, [5])
    NXB = 6
    xblk = [sb(f"xblk{i}", [128, D], F32, [0]) for i in range(NXB)]
    hn = [sb(f"hn{i}", [128, D], BF16, [0, 5]) for i in range(2)]
    junk = sb("junk", [128, D], BF16, [0, 5])
    WinX = sb("WinX", [128, 8, 1024], BF16, [0, 1])
    WinR = sb("WinR", [128, 8, 2056], BF16, [0, 1, 2])
    stage = [sb(f"stage{i}", [128, 2052], F32, [1]) for i in range(2)]
    cacc = [sb(f"cacc{i}", [128, 2048], F32, [1]) for i in range(2)]
    xbcT = sb("xbcT", [128, 8, T], BF16, [1, 2, 3])
    QKT = sb("QKT", [128, 8, T], BF16, [2, 3, 4])
    Vaug = sb("Vaug", [128, NB, 8, 65], BF16, [2, 3, 4])
    zs = sb("zs", [128, NB, 512], BF16, [2, 3])
    qk = [sb(f"qk{i}", [128, 1024], F32, [2]) for i in range(2)]
    tcb = sb("tcb", [128, 1024], F32, [2])
    tsb = sb("tsb", [128, 1024], F32, [2])
    rot = [sb(f"rot{i}", [128, 1024], BF16, [2]) for i in range(3)]
    b2s = sb("b2s", [128, 256], F32, [2])
    cwsw = sb("cwsw", [128, 2, 256], F32, [2])
    ztb = sb("ztb", [128, 512], F32, [2])
    qkw = sb("qkw", [128, 128], F32, [2])
    catT = sb("catT", [128, 8, T], BF16, [3, 4, 5])
    Wout = sb("Wout", [128, 8, D], BF16, [4, 5])
    xsB = [sb(f"xsB{i}", [128, 768], BF16, [3]) for i in range(2)]
    rhs_cs = [sb(f"rhs_cs{i}", [128, 1024], F32, [3]) for i in range(2)]
    dmat = [sb(f"dmat{i}", [128, 1024], F32, [3]) for i in range(2)]
    ebf = [sb(f"ebf{i}", [128, 1024], BF16, [3]) for i in range(2)]
    cbm = [sb(f"cbm{i}", [128, 256], BF16, [3]) for i in range(2)]
    Gm = [sb(f"Gm{i}", [128, 1024], BF16, [3]) for i in range(2)]
    c3s = [sb(f"c3s{i}", [128, 128], F32, [3]) for i in range(2)]
    xc = [sb(f"xc{i}", [128, 512], BF16, [3]) for i in range(2)]
    xcd = [sb(f"xcd{i}", [128, 512], BF16, [3]) for i in range(2)]
    xsd = [sb(f"xsd{i}", [128, 512], F32, [3]) for i in range(2)]
    yo_s = sb("yo_s", [128, 512], F32, [3])
    ybuf = sb("ybuf", [128, 512], F32, [3])
    t2 = sb("t2", [128, 512], F32, [3])
    yn2 = [sb(f"yn{i}", [128, 512], BF16, [3]) for i in range(2)]
    STf = sb("STf", [128, 512], F32, [3])
    STt = sb("STt", [128, 512], F32, [3])
    STb = sb("STb", [128, 512], BF16, [3])
    Pb = [sb(f"Pb{i}", [128, 512], BF16, [4]) for i in range(4)]
    Pm = [sb(f"Pm{i}", [128, 512], BF16, [4]) for i in range(4)]
    rden = [sb(f"rden{i}", [128, 4], F32, [4]) for i in range(2)]
    KTp = [sb(f"KTp{i}", [128, 4, T], BF16, [4]) for i in range(2)]
    o_tok = [sb(f"o_tok{i}", [128, 4, 512], BF16, [4]) for i in range(2)]
    Wm = [sb(f"Wm{i}", [128, 8, 1024], BF16, [4, 5, 6] if i < 2 else [6]) for i in range(3)]
    x1 = sb("x1", [128, NB, D], F32, [5, 6])
    hmT = sb("hmT", [128, 8, T], BF16, [5, 6])
    aT = sb("aT", [128, 8, T], BF16, [6])
    rbuf = [sb(f"rbuf{i}", [128, 512], F32, [6]) for i in range(3)]
    plan_done()

    QT = QKT[:, 0:4]
    KT = QKT[:, 4:8]

    def fcs(name):
        o, n = FC_LAY[name]
        return fc[:, o:o + n]

    ident = bc[:, 0:128]
    masks = bc[:, 128:128 + 2048]
    Uf = fcs('U')
    onesf = fcs('ones')
    cos_t = fcs('cos').rearrange("p (b f) -> p b f", f=32)
    sin_t = fcs('sin').rearrange("p (b f) -> p b f", f=32)
    convw = fcs('conv_w').rearrange("p (c t) -> p c t", t=4)
    convb = fcs('conv_b')

    ss = nstat[:, 0:256]
    rs = nstat[:, 256:512]
    ss2 = nstat[:, 512:768]
    rs2 = nstat[:, 768:1024]
    aneg = stats[:, 64:72]
    dt_all = stats[:, 128:256].rearrange("p (b h) -> p b h", h=8)
    adt_all = stats[:, 256:384].rearrange("p (b h) -> p b h", h=8)
    sm = stats[:, 384:512]


    win_v = win_d.rearrange("(kc p) n -> p kc n", p=128)
    wout_v = wout_d.rearrange("(kc p) n -> p kc n", p=128)
    wup_v = wup_d.rearrange("(kc p) n -> p kc n", p=128)
    wdn_v = wdn_d.rearrange("(f p) n -> p f n", p=128)

    P.dma('sp', xblk[0], x_d[0:128, :], nsem=12)
    P.dma('sp', nwA, nw_d[:, 0:1024], nsem=12)
    P.dma('sp', bc, bc_d, nsem=12)
    P.dma('sp', xblk[1], x_d[128:256, :], nsem=12)
    P.dma('sp', fc, fc_d, nsem=12)
    for b_ in range(2, NXB - 1):
        P.dma('sp', xblk[b_], x_d[b_ * 128:(b_ + 1) * 128, :], nsem=12)
    P.dma('pool', WinX, win_v[:, :, 2048:3072], after=[xblk[1]])
    P.memset('dve', ss, 0.0)
    P.memset('dve', ss2, 0.0)

    def rms_front(b, xin, wap, ssv, rsv, hnb):
        sc = slice(b * 16, b * 16 + 1)
        P.act(junk, xin, AF.Square, accum=ssv[:, sc])
        P.act(rsv[:, sc], ssv[:, sc], AF.Sqrt, bias=EPS, scale=1.0 / D)
        P.recip(rsv[:, sc], rsv[:, sc])
        P.stt('dve', hnb, xin, rsv[:, sc], wap, ALU.mult, ALU.mult)

    def rms_back(b, hnb, dstT, psbank):
        pt = bankbf(psbank)
        for kc in range(8):
            P.tr(pt[:, kc * 128:(kc + 1) * 128], hnb[:, kc * 128:(kc + 1) * 128], ident)
        P.cp('dve' if b % 2 == 0 else 'act', dstT[:, :, b * 128:(b + 1) * 128],
             pt.rearrange("p (k t) -> p k t", t=128))

    for b in range(NB):
        if b + NXB - 1 < NB:
            nb_ = b + NXB - 1
            P.dma('sp', xblk[nb_ % NXB], x_d[nb_ * 128:(nb_ + 1) * 128, :], nsem=12)
        rms_front(b, xblk[b % NXB], nwA, ss, rs, hn[b % 2])
        if b > 0:
            rms_back(b - 1, hn[(b - 1) % 2], hT, (b - 1) % 2)
    rms_back(NB - 1, hn[(NB - 1) % 2], hT, (NB - 1) % 2)
    for kc in range(0, 8, 4):
        P.dma('pool', WinR[:, kc:kc + 4, 0:2048], win_v[:, kc:kc + 4, 0:2048], after=[hT[:, 0, 1024:1152]])
    P.dma('pool', WinR[:, :, 2048:2056], win_v[:, :, 3072:3080])

    for i in range(2):
        P.memset('dve', stage[i][:, 0:3], 0.0)
    bk = 0
    for cc in range(8):
        stg = stage[cc % 2]
        acc = cacc[cc % 2]
        for g in range(4):
            ps = banks[2 + bk % 4]
            bk += 1
            for kc in range(8):
                P.mm(ps, WinX[:, kc, cc * 128:(cc + 1) * 128], hT[:, kc, g * 512:(g + 1) * 512],
                     start=(kc == 0), stop=(kc == 7))
            P.cp('act', stg[:, 3 + g * 512:3 + (g + 1) * 512], ps)
        P.act(acc, stg[:, 0:T], AF.Identity, bias=convb[:, cc:cc + 1], scale=convw[:, cc, 0:1])
        if cc > 0:
            P.act(xbcT[:, cc - 1, :], cacc[(cc - 1) % 2], AF.Silu)
        for tap in range(1, 4):
            P.stt('dve', acc, stg[:, tap:tap + T], convw[:, cc, tap:tap + 1], acc, ALU.mult, ALU.add)
    P.act(xbcT[:, 7, :], cacc[7 % 2], AF.Silu)

    P.act(aneg, fcs('alog'), AF.Exp)
    P.ts('dve', aneg, aneg, -1.0, None, ALU.mult)
    P.memset('pool', Vaug[:, :, :, 64:65], 1.0)
    P.ts('dve', qkw[:, 0:64], fcs('qw'), 0.125, None, ALU.mult)
    P.cp('dve', qkw[:, 64:128], fcs('kw'))

    def b2_front(b):
        blk = slice(b * 128, (b + 1) * 128)
        q_ = qk[b % 2]
        so = (b % 2) * 128
        zbank = banks[3] if b % 2 == 0 else banks[7]
        psdt = banks[4 if b % 2 == 0 else 6][:, 0:8]
        for kc in range(8):
            P.mm(psdt, hT[:, kc, blk], WinR[:, kc, 2048:2056], start=(kc == 0), stop=(kc == 7))
        for gi in range(4):
            ps = zbank if gi == 3 else banks[gi]
            for kc in range(8):
                P.mm(ps, hT[:, kc, blk], WinR[:, kc, gi * 512:(gi + 1) * 512], start=(kc == 0), stop=(kc == 7))
        xb_ = b2s[:, so + 0:so + 8]
        ax_ = b2s[:, so + 8:so + 16]
        ex_ = b2s[:, so + 16:so + 24]
        P.tt('dve', xb_, psdt, fcs('dtb'), ALU.add)
        P.stt('dve', ax_, xb_, -1.0, xb_, ALU.mult, ALU.max)
        P.cp('act', q_[:, 0:512], banks[0])
        P.cp('act', q_[:, 512:1024], banks[1])
        ssq = b2s[:, so + 32:so + 48]
        rq = b2s[:, so + 48:so + 64]
        P.act(tsb, q_, AF.Square)
        P.red('dve', ssq, tsb.rearrange("p (h d) -> p h d", d=64), ALU.add)
        P.cp('act', Vaug[:, b, :, 0:64], banks[2].rearrange("p (h d) -> p h d", d=64))
        P.act(ex_, ax_, AF.Exp, scale=-1.0)
        P.act(ex_, ex_, AF.Ln, bias=1.0)
        P.act(rq, ssq, AF.Ln, bias=EPS, scale=1.0 / 64)
        P.act(rq, rq, AF.Exp, scale=-0.5)
        P.act(ztb, zbank, AF.Exp, scale=-1.0)
        P.act(ztb, ztb, AF.Ln, bias=1.0)
        P.act(ztb, ztb, AF.Exp, scale=-1.0)
        P.stt('dve', dt_all[:, b, :], xb_, 0.0, ex_, ALU.max, ALU.add)
        P.tt('dve', adt_all[:, b, :], dt_all[:, b, :], aneg, ALU.mult)
        q3 = q_.rearrange("p (h d) -> p h d", d=64)
        P.tt('dve', q3, q3, rq.unsqueeze(2).to_broadcast([128, 16, 64]), ALU.mult)
        cw_ = cwsw[:, b % 2, 0:128]
        sw_ = cwsw[:, b % 2, 128:256]
        g4 = qkw.rearrange("p (w t f) -> p w t f", w=2, t=2)
        P.tt('pool', cw_.rearrange("p (w t f) -> p w t f", w=2, t=2), g4,
             cos_t[:, b, :].unsqueeze(1).unsqueeze(1).to_broadcast([128, 2, 2, 32]), ALU.mult)
        P.tt('pool', sw_.rearrange("p (w t f) -> p w t f", w=2, t=2), g4,
             sin_t[:, b, :].unsqueeze(1).unsqueeze(1).to_broadcast([128, 2, 2, 32]), ALU.mult)
        qv = q_.rearrange("p (w h d) -> p w h d", w=2, d=64)
        tcv = tcb.rearrange("p (w h d) -> p w h d", w=2, d=64)
        tsv = tsb.rearrange("p (w h d) -> p w h d", w=2, d=64)
        P.tt('dve', tcv, qv, cw_.rearrange("p (w d) -> p w d", d=64).unsqueeze(2).to_broadcast([128, 2, 8, 64]),
             ALU.mult)
        P.tt('pool', tsv, qv, sw_.rearrange("p (w d) -> p w d", d=64).unsqueeze(2).to_broadcast([128, 2, 8, 64]),
             ALU.mult)
        tc4 = tcb.rearrange("p (h t f) -> p h t f", t=2, f=32)
        ts4 = tsb.rearrange("p (h t f) -> p h t f", t=2, f=32)
        r4 = rot[b % 3].rearrange("p (h t f) -> p h t f", t=2, f=32)
        P.tt('dve', r4[:, :, 0, :], tc4[:, :, 0, :], ts4[:, :, 1, :], ALU.subtract)
        P.tt('dve', r4[:, :, 1, :], tc4[:, :, 1, :], ts4[:, :, 0, :], ALU.add)
        P.tt('dve', zs[:, b, :], zbank, ztb, ALU.mult)

    def b2_back(b):
        blk = slice(b * 128, (b + 1) * 128)
        pt = bankbf(5)
        for pr in range(8):
            P.tr(pt[:, pr * 128:(pr + 1) * 128], rot[b % 3][:, pr * 128:(pr + 1) * 128], ident)
        P.cp('act', QKT[:, :, blk], pt.rearrange("p (k t) -> p k t", t=128))

    for b in range(NB):
        b2_front(b)
        if b > 1:
            b2_back(b - 2)
    b2_back(NB - 2)
    b2_back(NB - 1)


    P.memset('dve', STf, 0.0)
    P.memset('pool', STb, 0.0)
    dskb = fcs('dsk')

    def ssd_f0(c):
        p = c % 2
        blk = slice(c * 128, (c + 1) * 128)
        P.tt('pool', rhs_cs[p].rearrange("p (h l) -> p h l", l=128),
             Uf.unsqueeze(1).to_broadcast([128, 8, 128]),
             adt_all[:, c, :].unsqueeze(2).to_broadcast([128, 8, 128]), ALU.mult)
        for j in range(2):
            P.mm(banks[1 + j], onesf, rhs_cs[p][:, j * 512:(j + 1) * 512])
        P.mm(banks[3][:, 0:8], Uf, adt_all[:, c, :])
        for g in range(2):
            P.mm(banks[3][:, 128 + g * 128:128 + (g + 1) * 128], xbcT[:, 4 + g, blk], xbcT[:, 6 + g, blk])

    def ssd_f1(c):
        p = c % 2
        blk = slice(c * 128, (c + 1) * 128)
        ptin = bankbf(0)
        for cc in range(4):
            P.tr(ptin[:, cc * 128:(cc + 1) * 128], xbcT[:, cc, blk], ident)
        for g in range(2):
            P.tr(ptin[:, 512 + g * 128:512 + (g + 1) * 128], xbcT[:, 4 + g, blk], ident)
        P.cp('act', xsB[p], ptin[:, 0:768])
        acs_t = c3s[p][:, 64:72]
        dte = c3s[p][:, 0:8]
        dec = c3s[p][:, 16:24]
        ea = c3s[p][:, 32:40]
        P.cp('dve', acs_t, banks[3][:, 0:8])
        for h in range(8):
            P.act(dmat[p][:, h * 128:(h + 1) * 128], banks[1 + h // 4][:, (h % 4) * 128:(h % 4 + 1) * 128],
                  AF.Relu, bias=acs_t[:, h:h + 1], scale=-1.0)
        P.act(ebf[p], dmat[p], AF.Exp, scale=-1.0)
        for j in range(2):
            last = banks[1 + j].rearrange("p (h l) -> p h l", l=128)[:, :, 127]
            P.tt('dve', dte[:, j * 4:(j + 1) * 4], last, acs_t[:, j * 4:(j + 1) * 4], ALU.subtract)
            P.act(dec[:, j * 4:(j + 1) * 4], last, AF.Exp)
        P.act(dte, dte, AF.Exp)
        P.act(ea, acs_t, AF.Exp)
        P.tt('dve', cbm[p].rearrange("p (g l) -> p g l", l=128),
             banks[3][:, 128:384].rearrange("p (g l) -> p g l", l=128),
             Uf.unsqueeze(1).to_broadcast([128, 2, 128]), ALU.mult)

    def ssd_f2(c):
        p = c % 2
        xs_tok = xsB[p][:, 0:512]
        dte = c3s[p][:, 0:8]
        P.tt('dve', Gm[p].rearrange("p (g r l) -> p g r l", r=4, l=128),
             ebf[p].rearrange("p (g r l) -> p g r l", r=4, l=128),
             cbm[p].rearrange("p (g l) -> p g l", l=128).unsqueeze(2).to_broadcast([128, 2, 4, 128]), ALU.mult)
        P.tt('pool', xc[p].rearrange("p (h d) -> p h d", d=64), xs_tok.rearrange("p (h d) -> p h d", d=64),
             dt_all[:, c, :].unsqueeze(2).to_broadcast([128, 8, 64]), ALU.mult)
        P.tt('pool', xcd[p].rearrange("p (h d) -> p h d", d=64), xc[p].rearrange("p (h d) -> p h d", d=64),
             dte.unsqueeze(2).to_broadcast([128, 8, 64]), ALU.mult)
        P.tt('pool', xsd[p].rearrange("p (h d) -> p h d", d=64), xs_tok.rearrange("p (h d) -> p h d", d=64),
             dskb.unsqueeze(2).to_broadcast([128, 8, 64]), ALU.mult)

    def ssd_f3(c):
        p = c % 2
        for h in range(8):
            P.mm(banks[4][:, h * 64:(h + 1) * 64], Gm[p][:, h * 128:(h + 1) * 128], xc[p][:, h * 64:(h + 1) * 64])
        for g in range(2):
            P.mm(banks[6][:, g * 256:(g + 1) * 256], xsB[p][:, 512 + g * 128:512 + (g + 1) * 128],
                 xcd[p][:, g * 256:(g + 1) * 256])

    def ssd_b0(c):
        blk = slice(c * 128, (c + 1) * 128)
        for g in range(2):
            P.mm(banks[5][:, g * 256:(g + 1) * 256], xbcT[:, 6 + g, blk], STb[:, g * 256:(g + 1) * 256])

    def ssd_b1(c):
        p = c % 2
        blk = slice(c * 128, (c + 1) * 128)
        dec = c3s[p][:, 16:24]
        ea = c3s[p][:, 32:40]
        P.tt('dve', yo_s.rearrange("p (h d) -> p h d", d=64), banks[5].rearrange("p (h d) -> p h d", d=64),
             ea.unsqueeze(2).to_broadcast([128, 8, 64]), ALU.mult)
        if c + 1 < NB:
            P.tt('dve', STt.rearrange("p (h d) -> p h d", d=64), STf.rearrange("p (h d) -> p h d", d=64),
                 dec.unsqueeze(2).to_broadcast([128, 8, 64]), ALU.mult)
            P.tt('dve', STf, STt, banks[6], ALU.add)
            P.cp('act', STb, STf)
        P.tt('dve', ybuf, yo_s, banks[4], ALU.add)
        P.tt('dve', ybuf, ybuf, xsd[p], ALU.add)
        P.tt('dve', ybuf, ybuf, zs[:, c, :], ALU.mult)
        ssy = c3s[p][:, 48:50]
        ry = c3s[p][:, 52:54]
        P.memset('dve', ssy, 0.0)
        for g in range(2):
            P.act(t2[:, g * 256:(g + 1) * 256], ybuf[:, g * 256:(g + 1) * 256], AF.Square, accum=ssy[:, g:g + 1])
        P.act(ry, ssy, AF.Ln, bias=EPS, scale=1.0 / 256)
        P.act(ry, ry, AF.Exp, scale=-0.5)

    def ssd_b2(c):
        p = c % 2
        yn = yn2[p]
        blk = slice(c * 128, (c + 1) * 128)
        ry = c3s[p][:, 52:54]
        for g in range(2):
            P.stt('dve', yn[:, g * 256:(g + 1) * 256], ybuf[:, g * 256:(g + 1) * 256], ry[:, g:g + 1],
                  fcs('ssd_w')[:, g * 256:(g + 1) * 256], ALU.mult, ALU.mult)
        ptout = bankbf(7)
        for j in range(4):
            P.tr(ptout[:, j * 128:(j + 1) * 128], yn[:, j * 128:(j + 1) * 128], ident)
        P.cp('act', catT[:, 4:8, blk], ptout[:, 0:512].rearrange("p (k t) -> p k t", t=128))

    ssd_f0(0)
    for c in range(NB):
        if c > 0:
            ssd_b0(c - 1)
        ssd_f1(c)
        if c + 1 < NB:
            ssd_f0(c + 1)
        if c > 0:
            ssd_b1(c - 1)
        ssd_f2(c)
        if c > 0:
            ssd_b2(c - 1)
        ssd_f3(c)
    ssd_b0(NB - 1)
    ssd_b1(NB - 1)
    ssd_b2(NB - 1)

    mlp_items = []
    for fg in range(4):
        mlp_items.append(('u', fg))
        mlp_items.append(('d', fg))

    def load_mlp(i):
        kind, fg = mlp_items[i]
        dst = Wm[i % 3]
        src = wup_v[:, :, fg * 1024:(fg + 1) * 1024] if kind == 'u' else wdn_v[:, fg * 8:(fg + 1) * 8, :]
        for k in range(0, 8, 4):
            P.dma('pool', dst[:, k:k + 4, :], src[:, k:k + 4, :])

    units = []
    for g in range(4):
        for h in range(8):
            for kb in range(4 * g + 4):
                units.append((g, h, kb))
    SB = [0, 1, 2, 7]
    LA = 3

    def att_scores(i):
        g, h, kb = units[i]
        pr, hp = h // 2, h % 2
        prt = slice(hp * 64, (hp + 1) * 64)
        r = max(0, kb - 4 * g)
        cols = slice(r * 128, 512)
        psS = banks[SB[i % 4]]
        P.mm(psS[:, cols], KTp[hp][:, pr, kb * 128:(kb + 1) * 128],
             QT[:, pr, g * 512 + r * 128:(g + 1) * 512])
        P.act(Pb[i % 4][:, cols], psS[:, cols], AF.Exp)
        d0 = 4 * g + r - kb
        nj = 4 - r
        P.tt('dve', Pm[i % 4][:, cols], Pb[i % 4][:, cols], masks[:, d0 * 128:(d0 + nj) * 128], ALU.mult)

    def att_pv(i):
        g, h, kb = units[i]
        gi = g * 8 + h
        acc3 = banks[3 + gi % 2][:, 0:260].rearrange("p (j e) -> p j e", e=65)
        r = max(0, kb - 4 * g)
        pm = Pm[i % 4]
        for jj in range(r, 4):
            P.mm(acc3[:, jj, :], pm[:, jj * 128:(jj + 1) * 128], Vaug[:, kb, h, :],
                 start=(kb == 0 and jj == 0), stop=(kb == 4 * g + jj), skip=True)
        if kb == 4 * g + 3:
            rd = rden[gi % 2]
            ot = o_tok[g % 2]
            P.recip(rd, acc3[:, :, 64])
            P.tt('dve', ot[:, :, h * 64:(h + 1) * 64], acc3[:, :, 0:64],
                 rd.unsqueeze(2).to_broadcast([128, 4, 64]), ALU.mult)
            if h == 7:
                for jj in range(4):
                    b = 4 * g + jj
                    pt = bankbf(5 + b % 2)
                    for j in range(4):
                        P.tr(pt[:, j * 128:(j + 1) * 128], ot[:, jj, j * 128:(j + 1) * 128], ident)
                    P.cp('dve', catT[:, 0:4, b * 128:(b + 1) * 128],
                         pt[:, 0:512].rearrange("p (k t) -> p k t", t=128))
                if g == 0:
                    load_mlp(0)
                    load_mlp(1)

    P.dma('pool', Wout, wout_v)
    P.add('act', lambda e: e.memzero(KTp[0][64:128]), [], [KTp[0][64:128]])
    P.add('act', lambda e: e.memzero(KTp[1][0:64]), [], [KTp[1][0:64]])
    for pr_ in range(4):
        P.cp('dve', KTp[0][0:64, pr_, :], KT[0:64, pr_, :])
        P.cp('dve' if pr_ < 2 else 'act', KTp[1][64:128, pr_, :], KT[64:128, pr_, :])
    for i in range(min(LA, len(units))):
        att_scores(i)
    for i in range(len(units)):
        if i + LA < len(units):
            att_scores(i + LA)
        att_pv(i)

    P.dma('sp', nwM, nw_d[:, 1024:2048])
    for b in range(NB):
        P.dma('sp', x1[:, b, :], x_d[b * 128:(b + 1) * 128, :])
    for b in range(NB):
        blk = slice(b * 128, (b + 1) * 128)
        for n in range(2):
            ps = banks[(b % 2) * 2 + n]
            for kc in range(8):
                P.mm(ps, catT[:, kc, blk], Wout[:, kc, n * 512:(n + 1) * 512], start=(kc == 0), stop=(kc == 7))
            P.tt('dve', x1[:, b, n * 512:(n + 1) * 512], x1[:, b, n * 512:(n + 1) * 512], ps, ALU.add)
        rms_front(b, x1[:, b, :], nwM, ss2, rs2, hn[b % 2])
        if b > 0:
            rms_back(b - 1, hn[(b - 1) % 2], hmT, 4 + (b - 1) % 2)
    rms_back(NB - 1, hn[(NB - 1) % 2], hmT, 4 + (NB - 1) % 2)

    ui = 0
    ri = 0
    for fg in range(4):
        if 2 * fg + 2 < 8:
            load_mlp(2 * fg + 2)
        Wu = Wm[(2 * fg) % 3]
        Wd = Wm[(2 * fg + 1) % 3]
        for tg in range(4):
            for fj in range(8):
                ps = banks[ui % 4]
                ui += 1
                for kc in range(8):
                    P.mm(ps, Wu[:, kc, fj * 128:(fj + 1) * 128], hmT[:, kc, tg * 512:(tg + 1) * 512],
                         start=(kc == 0), stop=(kc == 7))
                rb = rbuf[ri % 3]
                ri += 1
                P.act(rb, ps, AF.Relu)
                P.tt('dve', aT[:, fj, tg * 512:(tg + 1) * 512], rb, rb, ALU.mult)
        if 2 * fg + 3 < 8:
            load_mlp(2 * fg + 3)
        for b in range(NB):
            blk = slice(b * 128, (b + 1) * 128)
            for n in range(2):
                ps = banks[4 + (b % 2) * 2 + n]
                for fj in range(8):
                    P.mm(ps, aT[:, fj, blk], Wd[:, fj, n * 512:(n + 1) * 512], start=(fj == 0), stop=(fj == 7))
                P.tt('dve', x1[:, b, n * 512:(n + 1) * 512], x1[:, b, n * 512:(n + 1) * 512], ps, ALU.add)
            if fg == 3:
                P.dma('sp', out_d[b * 128:(b + 1) * 128, :], x1[:, b, :], final=True)

    P.emit()
    return nc


def _consts():
    t = np.arange(T, dtype=np.float64)
    half = 32
    inv = 10000.0 ** (-np.arange(half, dtype=np.float64) / half)
    ang = t[:, None] * inv[None, :]
    cos = np.cos(ang).astype(np.float32).reshape(NB, 128, 32).transpose(1, 0, 2).reshape(128, 512)
    sin = np.sin(ang).astype(np.float32).reshape(NB, 128, 32).transpose(1, 0, 2).reshape(128, 512)
    U = np.triu(np.ones((128, 128), np.float32))
    ones = np.ones((128, 128), np.float32)
    i = np.arange(128)
    masks = np.zeros((128, 16, 128), np.float32)
    for d in range(16):
        dl = 128 * d + i[None, :] - i[:, None]
        m = ((dl >= 0) & (dl <= 128)).astype(np.float32)
        m += ((dl >= 0) & (dl % 4 == 0) & (dl <= 512))
        m += ((dl >= 0) & (dl % 16 == 0) & (dl <= 2048))
        masks[:, d, :] = m
    ident = np.eye(128, dtype=np.float32)
    bcn = np.concatenate([ident, masks.reshape(128, 2048)], axis=1).astype(ml_dtypes.bfloat16)
    return cos, sin, U, ones, bcn


_CACHE = {}


def kernel(x, attn_norm_w, w_in, q_norm_w, k_norm_w, conv_w, conv_b, dt_bias, a_log, d_skip,
           ssd_norm_w, w_out, mlp_norm_w, w_up, w_down):
    f32 = np.float32
    cos, sin, U, ones, bcn = _consts()

    def bc(v):
        v = np.asarray(v, f32).reshape(1, -1)
        return np.broadcast_to(v, (128, v.shape[1]))

    cw = np.asarray(conv_w, f32)[0]
    cwp = cw.reshape(4, 8, 128).transpose(2, 1, 0).reshape(128, 32)
    cbp = np.asarray(conv_b, f32)[0].reshape(8, 128).T
    parts = {'ssd_w': bc(ssd_norm_w[0]),
             'qw': bc(q_norm_w[0]), 'kw': bc(k_norm_w[0]), 'conv_w': cwp, 'conv_b': cbp,
             'dtb': bc(dt_bias[0]), 'alog': bc(a_log[0]), 'dsk': bc(d_skip[0]),
             'cos': cos, 'sin': sin, 'U': U, 'ones': ones}
    fconst = np.zeros((128, FC_N), f32)
    for k, (o, n) in FC_LAY.items():
        fconst[:, o:o + n] = parts[k]
    if 'nc' not in _CACHE:
        _CACHE['nc'] = build_nc()
    nc = _CACHE['nc']
    x = np.asarray(x, f32)
    shared = {"w_in": np.ascontiguousarray(np.asarray(w_in, f32)[0]),
              "w_out": np.ascontiguousarray(np.asarray(w_out, f32)[0]),
              "w_up": np.ascontiguousarray(np.asarray(w_up, f32)[0]),
              "w_down": np.ascontiguousarray(np.asarray(w_down, f32)[0]),
              "fconst": fconst, "bconst": bcn,
              "normw": np.ascontiguousarray(np.concatenate([bc(attn_norm_w[0]), bc(mlp_norm_w[0])], axis=1))}
    in_maps = [dict(shared, x=np.ascontiguousarray(x[i])) for i in range(8)]
    res = run_bass_kernel_spmd(nc, in_maps, core_ids=list(range(8)))
    return np.stack([np.asarray(r["out"], f32) for r in res.results], axis=0)
```
